# Optimizing a Trainium2 kernel written in Bass

```python
import math
import jax, jax.numpy as jnp
from jax import lax
import numpy as np

D_MODEL = 1024
BATCH = 2
SEQ = 16384
DEPTH = 4
DEC_BATCH = 16
DEC_SEQ = 32
PAST_LEN = 1024

CHUNK = 64
N_HEADS = 8
HEAD_DIM = 64
ATT_W = N_HEADS * 2 * HEAD_DIM
CONV_W = D_MODEL // 2
CONV_K = 3
POOL_W = D_MODEL // 2
POOL_WINDOWS = (2, 4, 8, 16)
POOL_GROUPS = len(POOL_WINDOWS)
POOL_GW = POOL_W // POOL_GROUPS
POOL_HIST = max(POOL_WINDOWS) - 1
N_BRANCH = 3
IN_COLS = 3 * ATT_W + 3 * CONV_W + POOL_W + N_BRANCH * D_MODEL
BRANCH_ROWS = ATT_W + CONV_W + POOL_W
D_FF = 2816
PLE_DIM = 256
ROPE_THETA = 10000.0
Q_BLOCK = 128
RMS_EPS = 1e-6
N_NORMS = 8

kernel_name = 'hybrid_streaming_encoder_step'


def rms_norm(x, g):
    xf = x.astype(jnp.float32)
    y = xf * lax.rsqrt(jnp.mean(xf * xf, axis=-1, keepdims=True) + RMS_EPS)
    return (y * g.astype(jnp.float32)).astype(x.dtype)


def swiglu(x, w_up, w_down):
    gate, up = jnp.split(x @ w_up, 2, axis=-1)
    return (jax.nn.silu(gate) * up) @ w_down


def rope(x, pos):
    half = HEAD_DIM // 2
    inv = ROPE_THETA ** (-jnp.arange(half, dtype=jnp.float32) / half)
    ang = pos.astype(jnp.float32)[:, None] * inv[None, :]
    cos = jnp.cos(ang)[None, :, None, :]
    sin = jnp.sin(ang)[None, :, None, :]
    xf = x.astype(jnp.float32)
    x1, x2 = xf[..., :half], xf[..., half:]
    return jnp.concatenate([x1 * cos - x2 * sin, x2 * cos + x1 * sin], axis=-1).astype(x.dtype)


def diff_attn_core(q, k, v, lam, mask):
    s = jnp.einsum('bqhd,bkhd->bhqk', q, k).astype(jnp.float32) * (HEAD_DIM ** -0.5)
    if mask is not None:
        s = jnp.where(mask[None, None], s, -jnp.inf)
    p = jax.nn.softmax(s, axis=-1)
    b, _, tq, tk = p.shape
    p = p.reshape(b, N_HEADS, 2, tq, tk)
    a = (p[:, :, 0] - lam * p[:, :, 1]).astype(v.dtype)
    return jnp.einsum('bhqk,bkhe->bqhe', a, v)


def prompt_attention(q, k, v, lam):
    b, s = q.shape[:2]
    nb = s // Q_BLOCK
    qb = q.reshape(b, nb, Q_BLOCK, 2 * N_HEADS, HEAD_DIM).transpose(1, 0, 2, 3, 4)
    qpos = jnp.arange(s, dtype=jnp.int32).reshape(nb, Q_BLOCK)
    kchunk = jnp.arange(s, dtype=jnp.int32) // CHUNK

    def block(args):
        qblk, pos = args
        mask = (pos[:, None] // CHUNK) >= kchunk[None, :]
        return diff_attn_core(qblk, k, v, lam, mask)

    out = lax.map(block, (qb, qpos))
    return out.transpose(1, 0, 2, 3, 4).reshape(b, s, N_HEADS, 2 * HEAD_DIM)


def short_conv(z, hist, conv_w):
    t = z.shape[1]
    zext = jnp.concatenate([hist, z], axis=1)
    y = sum(conv_w[j] * zext[:, j:j + t] for j in range(CONV_K))
    return y, zext[:, -(CONV_K - 1):]


def pool_mixer(z, hist, hist_valid, pool_w, pool_scale):
    b, t, _ = z.shape
    zext = jnp.concatenate([hist, z], axis=1)
    zf = zext.astype(jnp.float32)
    c = jnp.concatenate([jnp.zeros_like(zf[:, :1]), jnp.cumsum(zf, axis=1)], axis=1)
    zcur = zf[:, POOL_HIST:]
    tpos = jnp.arange(t, dtype=jnp.int32)
    outs = []
    for g, w in enumerate(POOL_WINDOWS):
        sl = slice(g * POOL_GW, (g + 1) * POOL_GW)
        wsum = c[:, POOL_HIST + 1:POOL_HIST + 1 + t, sl] - c[:, POOL_HIST + 1 - w:POOL_HIST + 1 - w + t, sl]
        cnt = jnp.minimum(tpos + 1 + hist_valid, w).astype(jnp.float32)[None, :, None]
        outs.append(wsum / cnt - zcur[:, :, sl])
    d = jnp.stack(outs, axis=2).astype(z.dtype)
    y = jnp.einsum('btgc,gcd->btgd', d, pool_w).reshape(b, t, POOL_W) * pool_scale
    return y, zext[:, -POOL_HIST:]


def encoder_layer(li, x, p, pos, k_cache, v_cache, conv_hist, pool_hist, pool_valid,
                  norm_g, w_ffn_up, w_ffn_down, w_in, lam, subln_g, conv_w, pool_w, pool_scale,
                  w_branch, w_out, w_ple_up, w_ple_gate):
    b, t = x.shape[:2]
    h = x + 0.5 * rms_norm(swiglu(rms_norm(x, norm_g[0]), w_ffn_up[0], w_ffn_down[0]), norm_g[1])
    u = rms_norm(h, norm_g[2])
    offs = [ATT_W, 2 * ATT_W, 3 * ATT_W, 3 * ATT_W + CONV_W, 3 * ATT_W + 2 * CONV_W,
            3 * ATT_W + 3 * CONV_W, 3 * ATT_W + 3 * CONV_W + POOL_W]
    q, k, v, bg, cg, xin, zp, gates = jnp.split(u @ w_in, offs, axis=-1)
    q = rope(q.reshape(b, t, 2 * N_HEADS, HEAD_DIM), pos)
    k = rope(k.reshape(b, t, 2 * N_HEADS, HEAD_DIM), pos)
    v = v.reshape(b, t, N_HEADS, 2 * HEAD_DIM)
    lam_init = 0.8 - 0.6 * math.exp(-0.3 * li)
    lamf = lam.astype(jnp.float32)
    lam_full = jnp.exp(jnp.sum(lamf[0] * lamf[1])) - jnp.exp(jnp.sum(lamf[2] * lamf[3])) + lam_init
    if k_cache is None:
        att = prompt_attention(q, k, v, lam_full)
    else:
        att = diff_attn_core(q, jnp.concatenate([k_cache, k], axis=1),
                             jnp.concatenate([v_cache, v], axis=1), lam_full, None)
    att = rms_norm(att, subln_g.reshape(N_HEADS, 2 * HEAD_DIM)) * (1.0 - lam_init)
    att = att.reshape(b, t, ATT_W)
    y_conv, conv_new = short_conv(cg * xin, conv_hist, conv_w)
    y_conv = bg * y_conv
    y_pool, pool_new = pool_mixer(zp, pool_hist, pool_valid, pool_w, pool_scale)
    g = jax.nn.sigmoid(gates).reshape(b, t, N_BRANCH, D_MODEL)
    merged = (g[:, :, 0] * (att @ w_branch[:ATT_W])
              + g[:, :, 1] * (y_conv @ w_branch[ATT_W:ATT_W + CONV_W])
              + g[:, :, 2] * (y_pool @ w_branch[ATT_W + CONV_W:]))
    h = h + rms_norm(merged @ w_out, norm_g[3])
    h = h + 0.5 * rms_norm(swiglu(rms_norm(h, norm_g[4]), w_ffn_up[1], w_ffn_down[1]), norm_g[5])
    gate = jax.nn.sigmoid(rms_norm(h, norm_g[6]) @ w_ple_gate)
    h = h + rms_norm((p @ w_ple_up) * gate, norm_g[7])
    return h, k, v, conv_new, pool_new


def setup_inputs(seed: int = 0) -> dict:
    key = jax.random.key(seed)
    ks = jax.random.split(key, 24)
    f32 = jnp.float32

    def nrm(k, shape, scale):
        return jax.random.normal(k, shape, f32) * scale

    return {
        'x_prompt': nrm(ks[0], (BATCH, SEQ, D_MODEL), 1.0),
        'x_sample': nrm(ks[1], (DEC_BATCH, DEC_SEQ, D_MODEL), 1.0),
        'cache_k': nrm(ks[2], (DEPTH, DEC_BATCH, PAST_LEN, 2 * N_HEADS, HEAD_DIM), 1.0),
        'cache_v': nrm(ks[3], (DEPTH, DEC_BATCH, PAST_LEN, N_HEADS, 2 * HEAD_DIM), 1.0),
        'state_conv': nrm(ks[4], (DEPTH, DEC_BATCH, CONV_K - 1, CONV_W), 1.0),
        'state_pool': nrm(ks[5], (DEPTH, DEC_BATCH, POOL_HIST, POOL_W), 1.0),
        'p_prompt': nrm(ks[6], (DEPTH, BATCH, SEQ, PLE_DIM), 1.0),
        'p_sample': nrm(ks[7], (DEPTH, DEC_BATCH, DEC_SEQ, PLE_DIM), 1.0),
        'norm_g': 1.0 + nrm(ks[8], (DEPTH, N_NORMS, D_MODEL), 0.05),
        'w_ffn_up': nrm(ks[9], (DEPTH, 2, D_MODEL, 2 * D_FF), D_MODEL ** -0.5),
        'w_ffn_down': nrm(ks[10], (DEPTH, 2, D_FF, D_MODEL), D_FF ** -0.5),
        'w_in': nrm(ks[11], (DEPTH, D_MODEL, IN_COLS), D_MODEL ** -0.5),
        'lam': nrm(ks[12], (DEPTH, 4, HEAD_DIM), 0.1),
        'subln_g': 1.0 + nrm(ks[13], (DEPTH, ATT_W), 0.05),
        'conv_w': nrm(ks[14], (DEPTH, CONV_K, CONV_W), CONV_K ** -0.5),
        'pool_w': nrm(ks[15], (DEPTH, POOL_GROUPS, POOL_GW, POOL_GW), POOL_GW ** -0.5),
        'pool_scale': 1.0 + nrm(ks[16], (DEPTH, POOL_W), 0.1),
        'w_branch': nrm(ks[17], (DEPTH, BRANCH_ROWS, D_MODEL), ATT_W ** -0.5),
        'w_out': nrm(ks[18], (DEPTH, D_MODEL, D_MODEL), D_MODEL ** -0.5),
        'w_ple_up': nrm(ks[19], (DEPTH, PLE_DIM, D_MODEL), PLE_DIM ** -0.5),
        'w_ple_gate': nrm(ks[20], (DEPTH, D_MODEL, D_MODEL), D_MODEL ** -0.5),
    }


def reference(x_prompt, x_sample, cache_k, cache_v, state_conv, state_pool, p_prompt, p_sample,
              norm_g, w_ffn_up, w_ffn_down, w_in, lam, subln_g, conv_w, pool_w, pool_scale,
              w_branch, w_out, w_ple_up, w_ple_gate):
    b = x_prompt.shape[0]
    pos_p = jnp.arange(x_prompt.shape[1], dtype=jnp.int32)
    pos_s = PAST_LEN + jnp.arange(x_sample.shape[1], dtype=jnp.int32)
    zero_conv = jnp.zeros((b, CONV_K - 1, CONV_W), x_prompt.dtype)
    zero_pool = jnp.zeros((b, POOL_HIST, POOL_W), x_prompt.dtype)
    sample_pool_valid = min(PAST_LEN, POOL_HIST)
    hp, hs = x_prompt, x_sample
    kp_l, vp_l, cp_l, pp_l = [], [], [], []
    ks_l, vs_l, cs_l, ps_l = [], [], [], []
    for li in range(DEPTH):
        wl = (norm_g[li], w_ffn_up[li], w_ffn_down[li], w_in[li], lam[li], subln_g[li], conv_w[li],
              pool_w[li], pool_scale[li], w_branch[li], w_out[li], w_ple_up[li], w_ple_gate[li])
        hp, kp, vp, cp, pp = encoder_layer(li, hp, p_prompt[li], pos_p, None, None,
                                           zero_conv, zero_pool, 0, *wl)
        hs, ks_, vs_, cs_, ps_ = encoder_layer(li, hs, p_sample[li], pos_s, cache_k[li], cache_v[li],
                                               state_conv[li], state_pool[li], sample_pool_valid, *wl)
        kp_l.append(kp); vp_l.append(vp); cp_l.append(cp); pp_l.append(pp)
        ks_l.append(ks_); vs_l.append(vs_); cs_l.append(cs_); ps_l.append(ps_)
    return (hp, hs,
            jnp.stack(kp_l), jnp.stack(vp_l), jnp.stack(cp_l), jnp.stack(pp_l),
            jnp.stack(ks_l), jnp.stack(vs_l), jnp.stack(cs_l), jnp.stack(ps_l))
```

```python
import math
from contextlib import ExitStack

import numpy as np
import concourse.bass as bass
import concourse.mybir as mybir
from concourse.bass_utils import run_bass_kernel_spmd

F32 = mybir.dt.float32
BF16 = mybir.dt.bfloat16
ALU = mybir.AluOpType
AF = mybir.ActivationFunctionType

D = 1024
DFF = 2816
PLE = 256
PAST = 1024
EPS = 1e-6
SAME_ENG_SYNC = True


class Res:
    __slots__ = ("name", "lw", "rd")

    def __init__(self, name):
        self.name = name
        self.lw = None
        self.rd = {}


class Op:
    __slots__ = ("eng", "fn", "deps", "semkey", "sig", "cnt", "isdma", "cnt_order")


class Prog:
    ENG = ("pe", "act", "dve", "pool", "sp")

    def __init__(self):
        self.ops = {e: [] for e in self.ENG}
        self.dma_cnt = {}
        self.n = 0
        import os
        self.max_ops = int(os.environ.get("KSTOP", "1000000000"))
        self.trace = [] if os.environ.get("KTRACE") else None

    def op(self, eng, fn, reads=(), writes=(), dma_key=None, ndma=0):
        if self.n >= self.max_ops:
            return None
        o = Op()
        self.n += 1
        o.cnt_order = self.n
        o.eng = eng
        o.fn = fn
        o.isdma = dma_key is not None
        o.semkey = dma_key if o.isdma else eng
        deps = {}

        def add(d):
            if d is None:
                return
            if (not d.isdma) and (not o.isdma) and d.eng == eng:
                if eng == "pe" or not SAME_ENG_SYNC:
                    return
            k = d.semkey
            if k not in deps or deps[k].cnt_order < d.cnt_order:
                deps[k] = d

        for r in reads:
            add(r.lw)
        for w in writes:
            add(w.lw)
            for d in w.rd.values():
                add(d)
        o.deps = list(deps.values())
        for r in reads:
            r.rd[o.semkey] = o
        for w in writes:
            w.lw = o
            w.rd = {}
        if o.isdma:
            c = self.dma_cnt.get(dma_key, 0) + 16 * ndma
            self.dma_cnt[dma_key] = c
            o.cnt = c
            o.sig = True
        else:
            o.cnt = None
            o.sig = False
        self.ops[eng].append(o)
        if self.trace is not None:
            self.trace.append((self.n, eng, fn.__qualname__.split(".")[-3] if fn.__qualname__.count(".") >= 3 else fn.__qualname__, dma_key))
        return o

    def finalize(self):
        for e in self.ENG:
            for o in self.ops[e]:
                for d in o.deps:
                    if not d.isdma:
                        d.sig = True
        for e in self.ENG:
            c = 0
            for o in self.ops[e]:
                if not o.isdma:
                    if o.sig:
                        c += 1
                    o.cnt = c if o.sig else None

    def emit(self, eng, h, sems, final_waits=False):
        waited = {}
        for o in self.ops[eng]:
            need = {}
            for d in o.deps:
                v = d.cnt
                if need.get(d.semkey, 0) < v:
                    need[d.semkey] = v
            for k, v in need.items():
                if waited.get(k, 0) < v:
                    h.wait_ge(sems[k], v)
                    waited[k] = v
            r = o.fn(h)
            if o.isdma:
                for ins in r:
                    ins.then_inc(sems[o.semkey], 16)
            elif o.sig:
                r.then_inc(sems[eng], 1)
        if final_waits:
            for k, v in self.dma_cnt.items():
                if waited.get(k, 0) < v:
                    h.wait_ge(sems[k], v)


def panel_plan(L):
    plan = []
    for l in range(L):
        for f in range(2):
            for j in range(11):
                plan.append((("up", l, f, j), 8, 512))
            for m in range(8):
                plan.append((("dn", l, f, m), 22, 128))
        for j in range(2):
            plan.append((("inv", l, j), 8, 512))
        for h in range(8):
            plan.append((("inqk", l, h), 8, 512))
        for c in range(4):
            plan.append((("incp", l, c), 8, 512))
        for m in range(8):
            plan.append((("ing", l, m), 8, 384))
        for m in range(8):
            plan.append((("br", l, m), 16, 128))
        for j in range(2):
            plan.append((("out", l, j), 8, 512))
        for j in range(2):
            plan.append((("pg", l, j), 8, 512))
        for j in range(2):
            plan.append((("pu", l, j), 2, 512))
        plan.append((("pw", l), 4, 128))
    offs = {}
    o = 0
    for key, kc, w in plan:
        offs[key] = (o, kc, w)
        o += 128 * kc * w
    return plan, offs, o


def _pan(W, w):
    K, N = W.shape
    return W.reshape(K // 128, 128, N // w, w).transpose(2, 1, 0, 3)


def build_wflat(L, w_ffn_up, w_ffn_down, w_in, w_branch, w_out, w_ple_up, w_ple_gate, pool_w):
    plan, offs, tot = panel_plan(L)
    flat = np.empty(tot, np.float32)

    def put(key, arr):
        o, kc, w = offs[key]
        assert arr.shape == (128, kc, w), (key, arr.shape)
        flat[o:o + 128 * kc * w] = arr.reshape(-1)

    swap = (np.arange(64) + 32) % 64
    for l in range(L):
        for f in range(2):
            Wu = w_ffn_up[l, f]
            g = Wu[:, :DFF].reshape(8, 128, 11, 256)
            u = Wu[:, DFF:].reshape(8, 128, 11, 256)
            pu = np.concatenate([g, u], axis=-1).transpose(2, 1, 0, 3)
            for j in range(11):
                put(("up", l, f, j), pu[j])
            pd = _pan(w_ffn_down[l, f], 128)
            for m in range(8):
                put(("dn", l, f, m), pd[m])
        Wi = w_in[l]
        q = Wi[:, 0:1024]
        k = Wi[:, 1024:2048]
        v = Wi[:, 2048:3072]
        bg = Wi[:, 3072:3584]
        cg = Wi[:, 3584:4096]
        xin = Wi[:, 4096:4608]
        zp = Wi[:, 4608:5120]
        gates = Wi[:, 5120:8192]
        pv = _pan(v, 512)
        for j in range(2):
            put(("inv", l, j), pv[j])
        for h in range(8):
            cols = []
            for src in (q, k):
                blk = src[:, h * 128:(h + 1) * 128]
                sw = np.concatenate([blk[:, 0:64][:, swap], blk[:, 64:128][:, swap]], axis=1)
                cols += [blk, sw]
            put(("inqk", l, h), _pan(np.concatenate(cols, axis=1), 512)[0])
        for c in range(4):
            s = slice(c * 128, (c + 1) * 128)
            put(("incp", l, c), _pan(np.concatenate([cg[:, s], xin[:, s], bg[:, s], zp[:, s]], axis=1), 512)[0])
        for m in range(8):
            s = slice(m * 128, (m + 1) * 128)
            put(("ing", l, m), _pan(np.concatenate([gates[:, 0:1024][:, s], gates[:, 1024:2048][:, s],
                                                     gates[:, 2048:3072][:, s]], axis=1), 384)[0])
        pb = _pan(w_branch[l], 128)
        for m in range(8):
            put(("br", l, m), pb[m])
        po = _pan(w_out[l], 512)
        pg = _pan(w_ple_gate[l], 512)
        pu2 = _pan(w_ple_up[l], 512)
        for j in range(2):
            put(("out", l, j), po[j])
            put(("pg", l, j), pg[j])
            put(("pu", l, j), pu2[j])
        put(("pw", l), pool_w[l].transpose(1, 0, 2))
    return flat


NVL = 88


def build_vecs(L, norm_g, subln_g, pool_scale, conv_w):
    v = np.empty((128, L * NVL), np.float32)
    for l in range(L):
        b = l * NVL
        v[:, b:b + 64] = norm_g[l].reshape(8, 8, 128).transpose(2, 0, 1).reshape(128, 64)
        v[:, b + 64:b + 72] = subln_g[l].reshape(8, 128).T
        v[:, b + 72:b + 76] = pool_scale[l].reshape(4, 128).T
        v[:, b + 76:b + 88] = conv_w[l].reshape(3, 4, 128).transpose(2, 0, 1).reshape(128, 12)
    return v


def rope_tables(pos):
    half = 32
    inv = (np.float32(10000.0) ** (-np.arange(half, dtype=np.float32) / np.float32(half))).astype(np.float32)
    ang = pos.astype(np.float32)[None, :] * inv[:, None]
    c = np.cos(ang).astype(np.float32)
    s = np.sin(ang).astype(np.float32)
    cosT = np.concatenate([c, c, c, c], axis=0)
    sinT = np.concatenate([-s, s, -s, s], axis=0)
    return np.ascontiguousarray(cosT), np.ascontiguousarray(sinT)


def build_program(S, L):
    NT = S // 512
    plan, offs, NW = panel_plan(L)
    nc = bass.Bass("TRN2", target_bir_lowering=False)

    def din(name, shape, dt=F32):
        return nc.dram_tensor(name, list(shape), dt, kind="ExternalInput").ap()

    def dout(name, shape):
        return nc.dram_tensor(name, list(shape), F32, kind="ExternalOutput").ap()

    def dscr(name, shape, dt=BF16):
        return nc.dram_tensor(name, list(shape), dt, kind="Internal").ap()

    xp = din("xp", [S, D]); pp = din("pp", [L, S, PLE])
    xs = din("xs", [64, D]); pps = din("pps", [L, 64, PLE])
    ck = din("ck", [L, 2, PAST, D]); cv = din("cv", [L, 2, PAST, D])
    sc = din("sc", [L, 2, 2, 512]); spl = din("spl", [L, 2, 15, 512])
    wflat = din("wflat", [NW // 128, 128])
    vecs_d = din("vecs", [128, L * NVL]); lamb_d = din("lamb", [128, L * 256])
    cosT_d = din("cosT", [128, S]); sinT_d = din("sinT", [128, S])
    cosS_d = din("cosS", [128, 64]); sinS_d = din("sinS", [128, 64])
    invc_d = din("invc", [128, 64]); ident_d = din("ident", [128, 128])

    y_o = dout("y", [S, D]); ys_o = dout("ys", [64, D])
    ko = dout("ko", [L, S, D]); vo = dout("vo", [L, S, D])
    co = dout("co", [L, 2, 512]); po = dout("po", [L, 15, 512])
    kso = dout("kso", [L, 64, D]); vso = dout("vso", [L, 64, D])
    cso = dout("cso", [L, 2, 2, 512]); pso = dout("pso", [L, 2, 15, 512])

    wbf = dscr("wbf", [NW // 128, 128])
    kS = dscr("kS", [L, 8, 128, S])
    vS = dscr("vS", [L, 8, 128, S // 128, 128])
    kcS = dscr("kcS", [L, 2, 8, 128, PAST])
    vcS = dscr("vcS", [L, 2, 8, 128, PAST // 128, 128])

    P = Prog()
    es = ExitStack()

    SPEC = {}

    def sb(name, shape, dt=F32):
        off = (nc._sbuf_addr_for_side("left") + 31) // 32 * 32
        t = es.enter_context(nc.sbuf_tensor("sb_" + name, list(shape), dt))
        SPEC[name] = (list(shape), dt, off)
        return t

    hT = sb("hT", [128, 8, 512]); r_h = Res("hT")
    uT = sb("uT", [128, 8, 512], BF16); r_u = Res("uT")
    FA = sb("FA", [128, 4096]); r_fa = Res("FA")
    AR = sb("AR", [128, 16384], BF16)
    r_q = Res("q"); r_k = Res("k"); r_v = Res("v"); r_att = Res("att")
    r_hid = [r_q, r_k, r_v, r_att]
    ident = sb("ident", [128, 128]); ones = sb("ones", [128, 128], BF16)
    vecs = sb("vecs", [128, L * NVL]); lamb = sb("lamb", [128, L * 256])
    derived = sb("derived", [128, L * 16]); r_const = Res("const")
    halfg = sb("halfg", [128, L * 64])
    invc = sb("invc", [128, 64])
    cosb = sb("cosb", [128, 512]); sinb = sb("sinb", [128, 512]); r_rope = Res("rope")
    NWS = 3
    wslot = [sb(f"wslot{i}", [128, 4096], BF16) for i in range(NWS)]
    r_ws = [Res(f"ws{i}") for i in range(NWS)]
    NKV = 2
    kslot = [sb(f"kslot{i}", [128, 1024], BF16) for i in range(NKV)]
    vslot = [sb(f"vslot{i}", [128, 8, 128], BF16) for i in range(NKV)]
    r_ks = [Res(f"ks{i}") for i in range(NKV)]
    r_vs = [Res(f"vs{i}") for i in range(NKV)]
    pb = [sb(f"pb{b}", [128, 1024], BF16) for b in range(2)]
    r_pb = [Res(f"pb{b}") for b in range(2)]
    sq = [sb(f"sq{i}", [128, 512], BF16) for i in range(2)]; r_sq = [Res("sq0"), Res("sq1")]
    rstd = sb("rstd", [128, 512]); r_rstd = Res("rstd")
    tmpn = sb("tmpn", [128, 512]); r_tmpn = Res("tmpn")
    t1 = sb("t1", [128, 512]); r_t1 = Res("t1")
    t2 = sb("t2", [128, 512]); r_t2 = Res("t2")
    t3 = sb("t3", [128, 512]); r_t3 = Res("t3")
    t4 = sb("t4", [128, 512]); r_t4 = Res("t4")
    pstage = sb("pstage", [128, 4, 256]); r_pst = Res("pstage")
    pT = sb("pT", [128, 2, 512], BF16); r_pT = Res("pT")
    zbuf = sb("zbuf", [128, 4, 516]); r_z = Res("zbuf")
    zpbuf = sb("zpbuf", [128, 4, 530]); r_zp = Res("zpbuf")
    pa = sb("pa", [128, 530]); pbb = sb("pbb", [128, 530]); r_pa = Res("pa"); r_pbb = Res("pbb")
    cvt = sb("cvt", [128, 512]); r_cvt = Res("cvt")
    cv2 = sb("cv2", [128, 512]); r_cv2 = Res("cv2")
    yconv = sb("yconv", [128, 4, 512], BF16); r_yc = Res("yconv")
    dpool = sb("dpool", [128, 4, 512], BF16); r_dp = Res("dpool")
    ypool = sb("ypool", [128, 4, 512], BF16); r_yp = Res("ypool")
    zhist = sb("zhist", [128, L, 4, 2]); zphist = sb("zphist", [128, L, 4, 15])
    r_zh = [Res(f"zh{l}") for l in range(L)]; r_zph = [Res(f"zph{l}") for l in range(L)]

    psum_es = [ExitStack()]
    psw = [psum_es[0].enter_context(nc.psum_tensor(f"psw{i}", [128, 1024], F32)) for i in range(4)]
    psum = [psw[i // 2][:, (i % 2) * 512:(i % 2 + 1) * 512] for i in range(8)]
    r_ps = [Res(f"ps{i}") for i in range(8)]
    t12 = sb("t12", [128, 1024]); r_t12 = Res("t12")

    hid = AR[:, 0:22 * 512].rearrange("p (c t) -> p c t", t=512)
    qTb = AR[:, 0:4096].rearrange("p (c t) -> p c t", t=512)
    kTb = AR[:, 4096:8192].rearrange("p (c t) -> p c t", t=512)
    vb = AR[:, 8192:12288].rearrange("p (b d) -> p b d", d=1024)
    attT = AR[:, 12288:16384].rearrange("p (c t) -> p c t", t=512)
    merged = qTb
    yT = FA[:, :].rearrange("p (c t) -> p c t", t=512)
    tm = FA[:, :].rearrange("p (b d) -> p b d", d=1024)

    eps_ap = sb("eps_ap", [128, 1])
    nreb = [0]

    def rebind(tag):
        nonlocal hT, uT, FA, AR, ident, ones, vecs, lamb, derived, halfg, invc, cosb, sinb, wslot, kslot, vslot
        nonlocal pb, sq, rstd, tmpn, t1, t2, t3, t4, t12, pstage, pT, zbuf, zpbuf, pa, pbb, cvt, cv2
        nonlocal yconv, dpool, ypool, zhist, zphist, eps_ap, psw, psum, hid, qTb, kTb, vb, attT, merged, yT, tm
        nreb[0] += 1
        tg = f"{tag}_{nreb[0]}"

        def A(n):
            shape, dt, off = SPEC[n]
            return nc.alloc_sbuf_tensor_at(f"a_{n}_{tg}", shape, dt, offset=off)
        hT = A("hT"); uT = A("uT"); FA = A("FA"); AR = A("AR"); ident = A("ident"); ones = A("ones")
        vecs = A("vecs"); lamb = A("lamb"); derived = A("derived"); halfg = A("halfg"); invc = A("invc")
        cosb = A("cosb"); sinb = A("sinb")
        wslot = [A(f"wslot{i}") for i in range(NWS)]
        kslot = [A(f"kslot{i}") for i in range(NKV)]
        vslot = [A(f"vslot{i}") for i in range(NKV)]
        pb = [A(f"pb{b}") for b in range(2)]
        sq = [A(f"sq{i}") for i in range(2)]
        rstd = A("rstd"); tmpn = A("tmpn"); t1 = A("t1"); t2 = A("t2"); t3 = A("t3"); t4 = A("t4"); t12 = A("t12")
        pstage = A("pstage"); pT = A("pT"); zbuf = A("zbuf"); zpbuf = A("zpbuf"); pa = A("pa"); pbb = A("pbb")
        cvt = A("cvt"); cv2 = A("cv2"); yconv = A("yconv"); dpool = A("dpool"); ypool = A("ypool")
        zhist = A("zhist"); zphist = A("zphist"); eps_ap = A("eps_ap")
        psum_es[0].close()
        psum_es[0] = ExitStack()
        psw = [psum_es[0].enter_context(nc.psum_tensor(f"psw{i}_{tg}", [128, 1024], F32)) for i in range(4)]
        psum = [psw[i // 2][:, (i % 2) * 512:(i % 2 + 1) * 512] for i in range(8)]
        hid = AR[:, 0:22 * 512].rearrange("p (c t) -> p c t", t=512)
        qTb = AR[:, 0:4096].rearrange("p (c t) -> p c t", t=512)
        kTb = AR[:, 4096:8192].rearrange("p (c t) -> p c t", t=512)
        vb = AR[:, 8192:12288].rearrange("p (b d) -> p b d", d=1024)
        attT = AR[:, 12288:16384].rearrange("p (c t) -> p c t", t=512)
        merged = qTb
        yT = FA[:, :].rearrange("p (c t) -> p c t", t=512)
        tm = FA[:, :].rearrange("p (b d) -> p b d", d=1024)

    r_wbf = Res("wbf"); r_kcS = Res("kcS"); r_vcS = Res("vcS")
    r_kS = {}; r_vS = {}

    ENGNAME = {"act": "act", "dve": "dve", "pool": "pool"}

    def MM(ps_ap, pairs, reads, writes):
        pairs = list(pairs)

        def fn(h, ps_ap=ps_ap, pairs=pairs):
            n = len(pairs)
            for i, (l_, r_) in enumerate(pairs):
                ins = h.matmul(ps_ap, lhsT=l_, rhs=r_, start=(i == 0), stop=(i == n - 1))
            return ins
        return P.op("pe", fn, reads, writes)

    def MM1(ps_ap, l_, r_, start, stop, reads, writes):
        def fn(h, ps_ap=ps_ap, l_=l_, r_=r_, start=start, stop=stop):
            return h.matmul(ps_ap, lhsT=l_, rhs=r_, start=start, stop=stop)
        return P.op("pe", fn, reads, writes)

    def TR(items, reads, writes):
        items = list(items)

        def fn(h, items=items):
            for (o_, i_, id_) in items:
                ins = h.transpose(out=o_, in_=i_, identity=id_)
            return ins
        return P.op("pe", fn, reads, writes)

    def ACTF(out, in_, func, reads, writes, scale=None, bias=None):
        def fn(h, out=out, in_=in_, func=func, scale=scale, bias=bias):
            kw = {}
            if scale is not None:
                kw["scale"] = scale
            if bias is not None:
                kw["bias"] = bias
            return h.activation(out=out, in_=in_, func=func, **kw)
        return P.op("act", fn, reads, writes)

    def CP(eng, out, in_, reads, writes):
        def fn(h, out=out, in_=in_, eng=eng):
            if eng == "act":
                return h.copy(out=out, in_=in_)
            return h.tensor_copy(out=out, in_=in_)
        return P.op(eng, fn, reads, writes)

    def CPS(eng, pairs, reads, writes):
        pairs = list(pairs)

        def fn(h, pairs=pairs, eng=eng):
            for (o_, i_) in pairs:
                ins = h.copy(out=o_, in_=i_) if eng == "act" else h.tensor_copy(out=o_, in_=i_)
            return ins
        return P.op(eng, fn, reads, writes)

    def TT(eng, out, in0, in1, op, reads, writes):
        def fn(h, out=out, in0=in0, in1=in1, op=op):
            return h.tensor_tensor(out=out, in0=in0, in1=in1, op=op)
        return P.op(eng, fn, reads, writes)

    def STT(eng, out, in0, scalar, in1, op0, op1, reads, writes):
        def fn(h, out=out, in0=in0, scalar=scalar, in1=in1, op0=op0, op1=op1):
            return h.scalar_tensor_tensor(out=out, in0=in0, scalar=scalar, in1=in1, op0=op0, op1=op1)
        return P.op(eng, fn, reads, writes)

    def TS(eng, out, in0, s1, op0, reads, writes):
        def fn(h, out=out, in0=in0, s1=s1, op0=op0):
            return h.tensor_scalar(out=out, in0=in0, scalar1=s1, scalar2=None, op0=op0)
        return P.op(eng, fn, reads, writes)

    def RCP(out, in_, reads, writes):
        def fn(h, out=out, in_=in_):
            return h.reciprocal(out=out, in_=in_)
        return P.op("dve", fn, reads, writes)

    def MS(eng, ap, val, writes):
        def fn(h, ap=ap, val=val):
            return h.memset(ap, val)
        return P.op(eng, fn, (), writes)

    def load(key, out_ap, in_ap, reads=(), writes=(), slow=False):
        def fn(h, out_ap=out_ap, in_ap=in_ap, slow=slow):
            return [h.dma_start(out=out_ap, in_=in_ap, allow_slow_non_contiguous=slow)]
        return P.op("sp", fn, reads, writes, dma_key=key, ndma=1)

    def store(key, out_ap, in_ap, reads=(), writes=(), slow=False):
        def fn(h, out_ap=out_ap, in_ap=in_ap, slow=slow):
            return [h.dma_start(out=out_ap, in_=in_ap, allow_slow_non_contiguous=slow)]
        return P.op("sp", fn, reads, writes, dma_key=key, ndma=1)

    def DMAS(eng, key, pairs, reads, writes):
        pairs = list(pairs)

        def fn(h, pairs=pairs):
            return [h.dma_start(out=o_, in_=i_) for (o_, i_) in pairs]
        return P.op(eng, fn, reads, writes, dma_key=key, ndma=len(pairs))

    psrr = [0]

    def next_ps(lo=0, n=4):
        i = lo + psrr[0] % n
        psrr[0] += 1
        return i

    wrr = [0]

    def wpanel(key):
        o, kc, w = offs[key]
        s = wrr[0] % NWS
        wrr[0] += 1
        n = kc * w
        r0 = o // 128
        src = wbf[r0:r0 + n, :].rearrange("(p a) b -> p (a b)", p=128)
        load(f"w{s}", wslot[s][:, 0:n], src, reads=[r_wbf], writes=[r_ws[s]])
        return wslot[s][:, 0:n].rearrange("p (k w) -> p k w", w=w), r_ws[s]

    load("c_ident", ident[:], ident_d[:, :], writes=[r_const])
    load("c_vecs", vecs[:], vecs_d[:, :], writes=[r_const])
    load("c_lamb", lamb[:], lamb_d[:, :], writes=[r_const])
    load("c_invc", invc[:], invc_d[:, :], writes=[r_const])
    MS("dve", ones[:], 1.0, [r_const])
    MS("dve", eps_ap[:], EPS, [r_const])
    for l in range(L):
        lam_init = 0.8 - 0.6 * math.exp(-0.3 * l)
        b = l * NVL
        TS("dve", derived[:, l * 16:l * 16 + 8], vecs[:, b + 64:b + 72], float(1.0 - lam_init), ALU.mult,
           [r_const], [r_const])
        TS("dve", halfg[:, l * 64:(l + 1) * 64], vecs[:, b:b + 64], 0.5, ALU.mult, [r_const], [r_const])
        TT("dve", t1[:, 0:64], lamb[:, l * 256:l * 256 + 64], lamb[:, l * 256 + 64:l * 256 + 128], ALU.mult,
           [r_const], [r_t1])
        TT("dve", t1[:, 64:128], lamb[:, l * 256 + 128:l * 256 + 192], lamb[:, l * 256 + 192:l * 256 + 256], ALU.mult,
           [r_const, r_t1], [r_t1])

        def fnr(h):
            return h.tensor_reduce(out=t2[:, 0:2], in_=t1[:, 0:128].rearrange("p (a b) -> p a b", b=64),
                                   axis=mybir.AxisListType.X, op=ALU.add)
        P.op("dve", fnr, [r_t1], [r_t2])
        ACTF(t2[:, 2:4], t2[:, 0:2], AF.Exp, [r_t2], [r_t2])
        TT("dve", t2[:, 4:5], t2[:, 3:4], t2[:, 2:3], ALU.subtract, [r_t2], [r_t2])
        TS("dve", derived[:, l * 16 + 8:l * 16 + 9], t2[:, 4:5], float(-lam_init), ALU.add, [r_t2], [r_const, r_t2])

    NR = NW // 1024
    CH = 8192
    nch = (NR + CH - 1) // CH
    wbf2 = wbf[:, :].rearrange("(r k) c -> r (k c)", k=8)
    wfl2 = wflat[:, :].rearrange("(r k) c -> r (k c)", k=8)
    DMAS("pool", "cast", [(wbf2[i * CH:min(NR, (i + 1) * CH), :], wfl2[i * CH:min(NR, (i + 1) * CH), :])
                          for i in range(nch)], [], [r_wbf])
    DMAS("pool", "castv",
         [(vcS[l, b, hh], cv[l, b, :, hh * 128:(hh + 1) * 128].rearrange("(k p) e -> p k e", p=128))
          for l in range(L) for b in range(2) for hh in range(8)], [], [r_vcS])
    import os as _os
    DBG = int(_os.environ.get("KDBG", "9"))
    for l in range(L if DBG >= 2 else 0):
        for b in range(2):
            for half in range(2):
                load("ld_fa", tm[:, :, :],
                     ck[l, b, half * 512:(half + 1) * 512, :].rearrange("(k p) d -> p k d", p=128), writes=[r_fa])
                for hh in range(8):
                    pi = next_ps(0, 4)
                    TR([(psum[pi][:, blk * 128:(blk + 1) * 128], tm[:, blk, hh * 128:(hh + 1) * 128], ident[:, :])
                        for blk in range(4)], [r_fa, r_const], [r_ps[pi]])
                    CP("act" if hh % 2 == 0 else "dve", kTb[:, hh, :], psum[pi][:, :], [r_ps[pi]], [r_k])
                store("st_k", kcS[l, b, :, :, half * 512:(half + 1) * 512].rearrange("h p t -> p h t"),
                      kTb[:, :, :], reads=[r_k], writes=[r_kcS])

    def rms_stat(srcs, T, scale_div, src_res, dst_rstd, r_dst):
        n = len(srcs)
        pi = next_ps(4, 2)
        for c, src in enumerate(srcs):
            s = c % 2
            ACTF(sq[s][:, 0:T], src, AF.Square, list(src_res), [r_sq[s]])
            MM1(psum[pi][:, 0:T], ones[:, :], sq[s][:, 0:T], c == 0, c == n - 1, [r_sq[s]], [r_ps[pi]])
        ACTF(tmpn[:, 0:T], psum[pi][:, 0:T], AF.Sqrt, [r_ps[pi]], [r_tmpn], scale=1.0 / scale_div, bias=eps_ap[:, 0:1])
        RCP(dst_rstd[:, 0:T], tmpn[:, 0:T], [r_tmpn], [r_dst])

    def pre_norm(l, n_idx, T):
        rms_stat([hT[:, c, 0:T] for c in range(8)], T, float(D), [r_h], rstd, r_rstd)
        gb = l * NVL + n_idx * 8
        for c in range(8):
            if c % 2 == 0:
                STT("dve", uT[:, c, 0:T], hT[:, c, 0:T], vecs[:, gb + c:gb + c + 1], rstd[:, 0:T],
                    ALU.mult, ALU.mult, [r_h, r_rstd], [r_u])
            else:
                TS("pool", cv2[:, 0:T], hT[:, c, 0:T], vecs[:, gb + c:gb + c + 1], ALU.mult, [r_h], [r_cv2])
                TT("pool", uT[:, c, 0:T], cv2[:, 0:T], rstd[:, 0:T], ALU.mult, [r_cv2, r_rstd], [r_u])

    def post_norm_add(l, n_idx, T, half):
        rms_stat([yT[:, c, 0:T] for c in range(8)], T, float(D), [r_fa], rstd, r_rstd)
        gb = l * NVL + n_idx * 8
        for c in range(8):
            gap = (halfg[:, l * 64 + n_idx * 8 + c:l * 64 + n_idx * 8 + c + 1] if half
                   else vecs[:, gb + c:gb + c + 1])
            eng = "dve" if c % 2 == 0 else "pool"
            tt, rt = (t1, r_t1) if c % 2 == 0 else (t2, r_t2)
            if eng == "dve":
                STT(eng, tt[:, 0:T], yT[:, c, 0:T], gap, rstd[:, 0:T], ALU.mult, ALU.mult, [r_fa, r_rstd], [rt])
            else:
                TS(eng, tt[:, 0:T], yT[:, c, 0:T], gap, ALU.mult, [r_fa], [rt])
                TT(eng, tt[:, 0:T], tt[:, 0:T], rstd[:, 0:T], ALU.mult, [rt, r_rstd], [rt])
            TT(eng, hT[:, c, 0:T], hT[:, c, 0:T], tt[:, 0:T], ALU.add, [r_h, rt], [r_h])

    def ffn(l, f, T):
        for j in range(11):
            wp, rw = wpanel(("up", l, f, j))
            for jj in range(2):
                hc = 2 * j + jj
                pg = next_ps(0, 4)
                pu = next_ps(0, 4)
                MM(psum[pg][:, 0:T], [(wp[:, k, jj * 128:(jj + 1) * 128], uT[:, k, 0:T]) for k in range(8)],
                   [rw, r_u], [r_ps[pg]])
                MM(psum[pu][:, 0:T], [(wp[:, k, 256 + jj * 128:256 + (jj + 1) * 128], uT[:, k, 0:T]) for k in range(8)],
                   [rw, r_u], [r_ps[pu]])
                tt, rt = (t3, r_t3) if hc % 2 == 0 else (t4, r_t4)
                ACTF(tt[:, 0:T], psum[pg][:, 0:T], AF.Silu, [r_ps[pg]], [rt])
                TT("dve", hid[:, hc, 0:T], psum[pu][:, 0:T], tt[:, 0:T], ALU.mult, [r_ps[pu], rt], r_hid)
        for m in range(8):
            wp, rw = wpanel(("dn", l, f, m))
            pi = next_ps(0, 4)
            MM(psum[pi][:, 0:T], [(wp[:, k, :], hid[:, k, 0:T]) for k in range(22)], [rw] + r_hid, [r_ps[pi]])
            CP("act", yT[:, m, 0:T], psum[pi][:, 0:T], [r_ps[pi]], [r_fa])
        post_norm_add(l, 1 if f == 0 else 5, T, True)

    def transposes_in(src_tok_fn, ntoks, ncols, dst_fn, dst_res, src_res):
        T = sum(ntoks)
        for c in range(ncols):
            pi = next_ps(6, 2)
            items = []
            o = 0
            for blk, nt in enumerate(ntoks):
                items.append((psum[pi][:, o:o + nt], src_tok_fn(blk, c), ident[0:nt, 0:nt]))
                o += nt
            TR(items, list(src_res) + [r_const], [r_ps[pi]])
            CP("act" if c % 2 == 0 else "dve", dst_fn(c), psum[pi][:, 0:T], [r_ps[pi]], list(dst_res))

    def transposes_out(src_fn, ntoks, ncols, src_res, dst_col_fn, dst_res):
        for c in range(ncols):
            pi = next_ps(6, 2)
            items = []
            o = 0
            for blk, nt in enumerate(ntoks):
                items.append((psum[pi][0:nt, blk * 128:(blk + 1) * 128], src_fn(c)[:, o:o + nt], ident[:, :]))
                o += nt
            TR(items, list(src_res) + [r_const], [r_ps[pi]])
            CPS("act" if c % 2 == 0 else "dve",
                [(dst_col_fn(blk, nt, c), psum[pi][0:nt, blk * 128:(blk + 1) * 128]) for blk, nt in enumerate(ntoks)],
                [r_ps[pi]], list(dst_res))

    kvrr = [0]
    PO = [4, 5]
    PD = [6, 7]

    def attention(l, qsegs):
        ob = l * 16
        for hh in range(8):
            for seg in qsegs:
                q0, nq = seg["q0"], seg["nq"]
                comp = []
                for (kc0, nk, vp0, vblk, qv, sliver) in seg["cur"]:
                    comp.append(dict(nk=nk, qv=qv, sliver=sliver,
                                     kl=[kTb[half * 64:(half + 1) * 64, hh, kc0:kc0 + nk] for half in range(2)],
                                     rk=[r_k], vl=vb[vp0:vp0 + nk, vblk, hh * 128:(hh + 1) * 128], rv=[r_v]))
                pieces = seg["prev"]
                base = kvrr[0]
                kvrr[0] += len(pieces)

                def reg_load(p, pieces=pieces, base=base, hh=hh):
                    Kf, Vf, nb, deps = pieces[p]
                    s = (base + p) % NKV
                    load(f"kv_k{s}", kslot[s][:, 0:nb * 128], Kf(hh), reads=deps, writes=[r_ks[s]])
                    load(f"kv_v{s}", vslot[s][:, 0:nb, :], Vf(hh), reads=deps, writes=[r_vs[s]])

                for p, (Kf, Vf, nb, deps) in enumerate(pieces):
                    s = (base + p) % NKV
                    for bi in range(nb):
                        comp.append(dict(nk=128, qv=0, sliver=False, piece=p, last=(bi == nb - 1),
                                         kl=[kslot[s][half * 64:(half + 1) * 64, bi * 128:(bi + 1) * 128] for half in range(2)],
                                         rk=[r_ks[s]], vl=vslot[s][:, bi, :], rv=[r_vs[s]]))
                for p in range(min(NKV, len(pieces))):
                    reg_load(p)
                nblk = len(comp)

                def h2(ap, a, b_):
                    return ap.rearrange("p (h q) -> p h q", h=2)[:, :, a:b_]

                def issue_S(j):
                    bl = comp[j]
                    buf = j % 2
                    nk, a, b_ = bl["nk"], bl["qv"], nq
                    MM2 = [(psw[buf][0:nk, half * 512 + a:half * 512 + b_], bl["kl"][half],
                            qTb[half * 64:(half + 1) * 64, hh, q0 + a:q0 + b_]) for half in range(2)]

                    def fn(h, MM2=MM2):
                        for (o_, l_, r_) in MM2:
                            ins = h.matmul(o_, lhsT=l_, rhs=r_, start=True, stop=True)
                        return ins
                    P.op("pe", fn, bl["rk"] + [r_q], [r_ps[2 * buf], r_ps[2 * buf + 1]])
                    ACTF(h2(pb[buf][0:nk, :], a, b_), h2(psw[buf][0:nk, :], a, b_), AF.Exp,
                         [r_ps[2 * buf], r_ps[2 * buf + 1]], [r_pb[buf]], scale=0.125)
                    if bl["sliver"]:
                        MS("pool", h2(pb[buf][64:128, :], a, a + 64), 0.0, [r_pb[buf]])

                def issue_PV(j):
                    bl = comp[j]
                    buf = j % 2
                    nk, a, b_ = bl["nk"], bl["qv"], nq
                    items = []
                    for half in range(2):
                        items.append((psw[2][:, half * 512 + a:half * 512 + b_], bl["vl"],
                                      pb[buf][0:nk, half * 512 + a:half * 512 + b_]))
                        items.append((psw[3][:, half * 512 + a:half * 512 + b_], ones[0:nk, :],
                                      pb[buf][0:nk, half * 512 + a:half * 512 + b_]))

                    def fn(h, items=items, first=(j == 0), last=(j == nblk - 1)):
                        for (o_, l_, r_) in items:
                            ins = h.matmul(o_, lhsT=l_, rhs=r_, start=first, stop=last)
                        return ins
                    P.op("pe", fn, bl["rv"] + [r_pb[buf]], [r_ps[4], r_ps[5], r_ps[6], r_ps[7]])

                issue_S(0)
                for j in range(nblk):
                    if j + 1 < nblk:
                        issue_S(j + 1)
                    issue_PV(j)
                    if comp[j].get("last") and comp[j]["piece"] + NKV < len(pieces):
                        reg_load(comp[j]["piece"] + NKV)
                RCP(h2(t12[:, :], 0, nq), h2(psw[3][:, :], 0, nq), [r_ps[6], r_ps[7]], [r_t12])
                TT("dve", h2(t12[:, :], 0, nq), h2(psw[2][:, :], 0, nq), h2(t12[:, :], 0, nq), ALU.mult,
                   [r_ps[4], r_ps[5], r_t12], [r_t12])
                STT("dve", t3[:, 0:nq], t12[:, 512:512 + nq], derived[:, ob + 8:ob + 9], t12[:, 0:nq], ALU.mult, ALU.add,
                    [r_t12], [r_t3])
                rms_stat([t3[:, 0:nq]], nq, 128.0, [r_t3], t4, r_t4)
                STT("dve", attT[:, hh, q0:q0 + nq], t3[:, 0:nq], derived[:, ob + hh:ob + hh + 1], t4[:, 0:nq],
                    ALU.mult, ALU.mult, [r_t3, r_t4], [r_att])

    def run_tile(sample, ti):
        if sample:
            T, NSEG, TS_, ntoks, PB = 64, 2, 32, [32, 32], 32
        else:
            T, NSEG, TS_, ntoks, PB = 512, 1, 512, [128] * 4, 128
        NBK = len(ntoks)
        t0 = ti * 512
        SEGW_Z = 2 + TS_
        SEGW_P = 15 + TS_

        def tokview(dr):
            return dr.rearrange("(k p) d -> p k d", p=PB)

        if sample:
            load("ld_fa", tm[0:PB, 0:NBK, :], tokview(xs[:, :]), writes=[r_fa])
            load("ld_rope", cosb[:, 0:64], cosS_d[:, :], writes=[r_rope])
            load("ld_rope", sinb[:, 0:64], sinS_d[:, :], writes=[r_rope])
        else:
            load("ld_fa", tm[:, :, :], tokview(xp[t0:t0 + 512, :]), writes=[r_fa])
            load("ld_rope", cosb[:, :], cosT_d[:, t0:t0 + 512], writes=[r_rope])
            load("ld_rope", sinb[:, :], sinT_d[:, t0:t0 + 512], writes=[r_rope])
        transposes_in(lambda blk, c: tm[0:ntoks[blk], blk, c * 128:(c + 1) * 128], ntoks, 8,
                      lambda c: hT[:, c, 0:T], [r_h], [r_fa])

        def zv(c4, off, n=TS_):
            return zbuf[:, c4, 0:NSEG * SEGW_Z].rearrange("p (s w) -> p s w", w=SEGW_Z)[:, :, off:off + n]

        def zpv(c4, off, n=TS_):
            return zpbuf[:, c4, 0:NSEG * SEGW_P].rearrange("p (s w) -> p s w", w=SEGW_P)[:, :, off:off + n]

        def v3(buf, off, n):
            return buf[:, 0:NSEG * SEGW_P].rearrange("p (s w) -> p s w", w=SEGW_P)[:, :, off:off + n]

        def tv(ap2d):
            return ap2d.rearrange("p (s w) -> p s w", w=TS_)

        for l in range(L):
            if not _os.environ.get("KNOREB"):
                rebind(f"{'s' if sample else 'p'}{ti}_{l}")
            vb_ = l * NVL
            cwb = vb_ + 76
            pre_norm(l, 0, T)
            ffn(l, 0, T)
            pre_norm(l, 2, T)
            if sample:
                load("ld_p", pstage[0:PB, 0:NBK, :], tokview(pps[l, :, :]), writes=[r_pst])
            else:
                load("ld_p", pstage[:, :, :], tokview(pp[l, t0:t0 + 512, :]), writes=[r_pst])
            for j in range(2):
                wp, rw = wpanel(("inv", l, j))
                o = 0
                for blk, nt in enumerate(ntoks):
                    pi = next_ps(0, 4)
                    MM(psum[pi][0:nt, :], [(uT[:, k, o:o + nt], wp[:, k, :]) for k in range(8)], [rw, r_u], [r_ps[pi]])
                    CP("act", tm[0:nt, blk, j * 512:(j + 1) * 512], psum[pi][0:nt, :], [r_ps[pi]], [r_fa])
                    CP("dve", vb[0:nt, blk, j * 512:(j + 1) * 512], tm[0:nt, blk, j * 512:(j + 1) * 512], [r_fa], [r_v])
                    o += nt
            if sample:
                store("st_fa", tokview(vso[l, :, :]), tm[0:PB, 0:NBK, :], reads=[r_fa])
            else:
                store("st_fa", tokview(vo[l, t0:t0 + 512, :]), tm[:, :, :], reads=[r_fa])
                rv = Res(f"vS{l}_{ti}")
                r_vS[(l, ti)] = rv
                DMAS("sp", "st_v", [(vS[l, hh, :, ti * 4:(ti + 1) * 4, :], vb[:, :, hh * 128:(hh + 1) * 128])
                                    for hh in range(8)], [r_v], [rv])
            for hh in range(8):
                wp, rw = wpanel(("inqk", l, hh))
                pis = []
                for jj in range(4):
                    pi = next_ps(0, 4)
                    pis.append(pi)
                    MM(psum[pi][:, 0:T], [(wp[:, k, jj * 128:(jj + 1) * 128], uT[:, k, 0:T]) for k in range(8)],
                       [rw, r_u], [r_ps[pi]])
                TT("dve", t1[:, 0:T], psum[pis[0]][:, 0:T], cosb[:, 0:T], ALU.mult, [r_ps[pis[0]], r_rope], [r_t1])
                TT("dve", t2[:, 0:T], psum[pis[1]][:, 0:T], sinb[:, 0:T], ALU.mult, [r_ps[pis[1]], r_rope], [r_t2])
                TT("pool", qTb[:, hh, 0:T], t1[:, 0:T], t2[:, 0:T], ALU.add, [r_t1, r_t2], [r_q])
                TT("dve", t3[:, 0:T], psum[pis[2]][:, 0:T], cosb[:, 0:T], ALU.mult, [r_ps[pis[2]], r_rope], [r_t3])
                TT("dve", t4[:, 0:T], psum[pis[3]][:, 0:T], sinb[:, 0:T], ALU.mult, [r_ps[pis[3]], r_rope], [r_t4])
                TT("pool", cvt[:, 0:T], t3[:, 0:T], t4[:, 0:T], ALU.add, [r_t3, r_t4], [r_cvt])
                CP("pool", kTb[:, hh, 0:T], cvt[:, 0:T], [r_cvt], [r_k])
                pi = next_ps(6, 2)
                items = []
                o = 0
                for blk, nt in enumerate(ntoks):
                    items.append((psum[pi][0:nt, blk * 128:(blk + 1) * 128], cvt[:, o:o + nt], ident[:, :]))
                    o += nt
                TR(items, [r_cvt, r_const], [r_ps[pi]])
                CPS("act", [(tm[0:nt, blk, hh * 128:(hh + 1) * 128], psum[pi][0:nt, blk * 128:(blk + 1) * 128])
                            for blk, nt in enumerate(ntoks)], [r_ps[pi]], [r_fa])
            if sample:
                store("st_fa", tokview(kso[l, :, :]), tm[0:PB, 0:NBK, :], reads=[r_fa])
            else:
                store("st_fa", tokview(ko[l, t0:t0 + 512, :]), tm[:, :, :], reads=[r_fa])
                rk = Res(f"kS{l}_{ti}")
                r_kS[(l, ti)] = rk
                store("st_k", kS[l, :, :, t0:t0 + 512].rearrange("h p t -> p h t"), kTb[:, :, :], reads=[r_k], writes=[rk])
            transposes_in(lambda blk, c: pstage[0:ntoks[blk], blk, c * 128:(c + 1) * 128], ntoks, 2,
                          lambda c: pT[:, c, 0:T], [r_pT], [r_pst])
            if sample:
                for b in range(2):
                    for c4 in range(4):
                        load("ld_st", zbuf[:, c4, b * SEGW_Z:b * SEGW_Z + 2],
                             sc[l, b, :, c4 * 128:(c4 + 1) * 128].rearrange("r p -> p r"), writes=[r_z], slow=True)
                        load("ld_st", zpbuf[:, c4, b * SEGW_P:b * SEGW_P + 15],
                             spl[l, b, :, c4 * 128:(c4 + 1) * 128].rearrange("r p -> p r"), writes=[r_zp], slow=True)
            else:
                if ti == 0:
                    MS("pool", zhist[:, l, :, :], 0.0, [r_zh[l]])
                    MS("pool", zphist[:, l, :, :], 0.0, [r_zph[l]])
                CP("pool", zbuf[:, :, 0:2], zhist[:, l, :, :], [r_zh[l]], [r_z])
                CP("pool", zpbuf[:, :, 0:15], zphist[:, l, :, :], [r_zph[l]], [r_zp])
            for c4 in range(4):
                wp, rw = wpanel(("incp", l, c4))
                pis = []
                for jj in range(4):
                    pi = next_ps(0, 4)
                    pis.append(pi)
                    MM(psum[pi][:, 0:T], [(wp[:, k, jj * 128:(jj + 1) * 128], uT[:, k, 0:T]) for k in range(8)],
                       [rw, r_u], [r_ps[pi]])
                CP("act", t1[:, 0:T], psum[pis[0]][:, 0:T], [r_ps[pis[0]]], [r_t1])
                TT("dve", zv(c4, 2), tv(psum[pis[1]][:, 0:T]), tv(t1[:, 0:T]), ALU.mult, [r_ps[pis[1]], r_t1], [r_z])
                CP("act", t2[:, 0:T], psum[pis[2]][:, 0:T], [r_ps[pis[2]]], [r_t2])
                CP("act", zpv(c4, 15), tv(psum[pis[3]][:, 0:T]), [r_ps[pis[3]]], [r_zp])
                TS("pool", tv(cvt[:, 0:T]), zv(c4, 0), vecs[:, cwb + c4:cwb + c4 + 1], ALU.mult, [r_z], [r_cvt])
                TS("pool", tv(cv2[:, 0:T]), zv(c4, 1), vecs[:, cwb + 4 + c4:cwb + 5 + c4], ALU.mult, [r_z], [r_cv2])
                TT("pool", cvt[:, 0:T], cvt[:, 0:T], cv2[:, 0:T], ALU.add, [r_cvt, r_cv2], [r_cvt])
                TS("pool", tv(cv2[:, 0:T]), zv(c4, 2), vecs[:, cwb + 8 + c4:cwb + 9 + c4], ALU.mult, [r_z], [r_cv2])
                TT("pool", cvt[:, 0:T], cvt[:, 0:T], cv2[:, 0:T], ALU.add, [r_cvt, r_cv2], [r_cvt])
                TT("pool", yconv[:, c4, 0:T], cvt[:, 0:T], t2[:, 0:T], ALU.mult, [r_cvt, r_t2], [r_yc])
                steps = c4 + 1
                bufs = [(pa, r_pa), (pbb, r_pbb)]
                sh = 1
                cur_lo = 0
                for st in range(steps):
                    new_lo = 15 if st == steps - 1 else cur_lo + sh
                    n = SEGW_P - new_lo
                    dbuf, dres = bufs[st % 2]
                    if st == 0:
                        in_a, in_b, rr = zpv(c4, new_lo, n), zpv(c4, new_lo - sh, n), [r_zp]
                    else:
                        sbuf_, sres = bufs[(st - 1) % 2]
                        in_a, in_b, rr = v3(sbuf_, new_lo, n), v3(sbuf_, new_lo - sh, n), [sres]
                    TT("dve", v3(dbuf, new_lo, n), in_a, in_b, ALU.add, rr, [dres])
                    cur_lo = new_lo
                    sh *= 2
                fbuf, fres = bufs[(steps - 1) % 2]
                wdw = float(2 ** (c4 + 1))
                STT("dve", tv(dpool[:, c4, 0:T]), v3(fbuf, 15, TS_), 1.0 / wdw, zpv(c4, 15), ALU.mult, ALU.subtract,
                    [fres, r_zp], [r_dp])
                if (not sample) and ti == 0:
                    TT("dve", t3[:, 0:16], fbuf[:, 15:31], invc[:, c4 * 16:(c4 + 1) * 16], ALU.mult, [fres], [r_t3])
                    TT("dve", dpool[:, c4, 0:16], t3[:, 0:16], zpbuf[:, c4, 15:31], ALU.subtract, [r_t3, r_zp], [r_dp])
            if sample:
                for b in range(2):
                    for c4 in range(4):
                        store("st_z", cso[l, b, :, c4 * 128:(c4 + 1) * 128].rearrange("r p -> p r"),
                              zbuf[:, c4, b * SEGW_Z + TS_:b * SEGW_Z + TS_ + 2], reads=[r_z], slow=True)
                        store("st_zp", pso[l, b, :, c4 * 128:(c4 + 1) * 128].rearrange("r p -> p r"),
                              zpbuf[:, c4, b * SEGW_P + TS_:b * SEGW_P + TS_ + 15], reads=[r_zp], slow=True)
            else:
                CP("pool", zhist[:, l, :, :], zbuf[:, :, 512:514], [r_z], [r_zh[l]])
                CP("pool", zphist[:, l, :, :], zpbuf[:, :, 512:527], [r_zp], [r_zph[l]])
                if ti == NT - 1:
                    for c4 in range(4):
                        store("st_z", co[l, :, c4 * 128:(c4 + 1) * 128].rearrange("r p -> p r"),
                              zbuf[:, c4, 512:514], reads=[r_z], slow=True)
                        store("st_zp", po[l, :, c4 * 128:(c4 + 1) * 128].rearrange("r p -> p r"),
                              zpbuf[:, c4, 512:527], reads=[r_zp], slow=True)
            wpw, rwpw = wpanel(("pw", l))
            for c4 in range(4):
                pi = next_ps(0, 4)
                MM1(psum[pi][:, 0:T], wpw[:, c4, :], dpool[:, c4, 0:T], True, True, [rwpw, r_dp], [r_ps[pi]])
                TS("dve", ypool[:, c4, 0:T], psum[pi][:, 0:T], vecs[:, vb_ + 72 + c4:vb_ + 73 + c4], ALU.mult,
                   [r_ps[pi]], [r_yp])
            if sample:
                qsegs = []
                for b in range(2):
                    qsegs.append(dict(
                        q0=b * 32, nq=32, cur=[(b * 32, 32, 0, b, 0, False)],
                        prev=[((lambda hh, l=l, b=b: kcS[l, b, hh, :, :]),
                               (lambda hh, l=l, b=b: vcS[l, b, hh, :, :, :]), 8, [r_kcS, r_vcS])]))
            else:
                prev = []
                nprev = ti * 4
                a = 0
                while a < nprev:
                    nb = min(8, nprev - a)
                    deps = [r_kS[(l, tt)] for tt in range(a // 4, (a + nb + 3) // 4)] + \
                           [r_vS[(l, tt)] for tt in range(a // 4, (a + nb + 3) // 4)]
                    prev.append(((lambda hh, l=l, a=a, nb=nb: kS[l, hh, :, a * 128:(a + nb) * 128]),
                                 (lambda hh, l=l, a=a, nb=nb: vS[l, hh, :, a:a + nb, :]), nb, deps))
                    a += nb
                qsegs = [dict(q0=0, nq=512, cur=[(r * 128, 128, 0, r, r * 128, True) for r in range(4)], prev=prev)]
            attention(l, qsegs)
            for m in range(8):
                wb, rwb = wpanel(("br", l, m))
                wg, rwg = wpanel(("ing", l, m))
                pA = next_ps(0, 4); pB = next_ps(0, 4); pC = next_ps(0, 4)
                MM(psum[pA][:, 0:T], [(wb[:, k, :], attT[:, k, 0:T]) for k in range(8)], [rwb, r_att], [r_ps[pA]])
                MM(psum[pB][:, 0:T], [(wb[:, 8 + k, :], yconv[:, k, 0:T]) for k in range(4)], [rwb, r_yc], [r_ps[pB]])
                MM(psum[pC][:, 0:T], [(wb[:, 12 + k, :], ypool[:, k, 0:T]) for k in range(4)], [rwb, r_yp], [r_ps[pC]])
                tts = [(t1, r_t1), (t2, r_t2), (t3, r_t3)]
                for gi, pX in enumerate((pA, pB, pC)):
                    pG = next_ps(4, 2)
                    MM(psum[pG][:, 0:T], [(wg[:, k, gi * 128:(gi + 1) * 128], uT[:, k, 0:T]) for k in range(8)],
                       [rwg, r_u], [r_ps[pG]])
                    tt, rt = tts[gi]
                    ACTF(tt[:, 0:T], psum[pG][:, 0:T], AF.Sigmoid, [r_ps[pG]], [rt])
                    TT("dve", tt[:, 0:T], psum[pX][:, 0:T], tt[:, 0:T], ALU.mult, [r_ps[pX], rt], [rt])
                TT("pool", t1[:, 0:T], t1[:, 0:T], t2[:, 0:T], ALU.add, [r_t1, r_t2], [r_t1])
                TT("pool", merged[:, m, 0:T], t1[:, 0:T], t3[:, 0:T], ALU.add, [r_t1, r_t3], [r_q])
            for j in range(2):
                wp, rw = wpanel(("out", l, j))
                for jj in range(4):
                    m = j * 4 + jj
                    pi = next_ps(0, 4)
                    MM(psum[pi][:, 0:T], [(wp[:, k, jj * 128:(jj + 1) * 128], merged[:, k, 0:T]) for k in range(8)],
                       [rw, r_q], [r_ps[pi]])
                    CP("act", yT[:, m, 0:T], psum[pi][:, 0:T], [r_ps[pi]], [r_fa])
            post_norm_add(l, 3, T, False)
            pre_norm(l, 4, T)
            ffn(l, 1, T)
            pre_norm(l, 6, T)
            for j in range(2):
                wg, rwg = wpanel(("pg", l, j))
                wu, rwu = wpanel(("pu", l, j))
                for jj in range(4):
                    m = j * 4 + jj
                    pG = next_ps(0, 4); pU = next_ps(0, 4)
                    MM(psum[pG][:, 0:T], [(wg[:, k, jj * 128:(jj + 1) * 128], uT[:, k, 0:T]) for k in range(8)],
                       [rwg, r_u], [r_ps[pG]])
                    MM(psum[pU][:, 0:T], [(wu[:, k, jj * 128:(jj + 1) * 128], pT[:, k, 0:T]) for k in range(2)],
                       [rwu, r_pT], [r_ps[pU]])
                    tt, rt = (t1, r_t1) if m % 2 == 0 else (t2, r_t2)
                    ACTF(tt[:, 0:T], psum[pG][:, 0:T], AF.Sigmoid, [r_ps[pG]], [rt])
                    TT("dve", yT[:, m, 0:T], psum[pU][:, 0:T], tt[:, 0:T], ALU.mult, [r_ps[pU], rt], [r_fa])
            post_norm_add(l, 7, T, False)
        transposes_out(lambda c: hT[:, c, 0:T], ntoks, 8, [r_h],
                       lambda blk, nt, c: tm[0:nt, blk, c * 128:(c + 1) * 128], [r_fa])
        if sample:
            store("st_fa", tokview(ys_o[:, :]), tm[0:PB, 0:NBK, :], reads=[r_fa])
        else:
            store("st_fa", tokview(y_o[t0:t0 + 512, :]), tm[:, :, :], reads=[r_fa])

    if DBG >= 3:
        run_tile(True, 0)
    for ti in range(NT if DBG >= 4 else 0):
        run_tile(False, ti)

    P.finalize()
    keys = list(Prog.ENG) + sorted(P.dma_cnt.keys())
    sems = {k: es.enter_context(nc.semaphore(f"s_{k}")) for k in keys}
    with nc.allow_non_contiguous_dma("small state transposes"):
        with nc.Block() as block:
            @block.tensor
            def _(h):
                P.emit("pe", h, sems)

            @block.scalar
            def _(h):
                P.emit("act", h, sems)

            @block.vector
            def _(h):
                P.emit("dve", h, sems)

            @block.gpsimd
            def _(h):
                P.emit("pool", h, sems)

            @block.sync
            def _(h):
                P.emit("sp", h, sems, final_waits=True)
    psum_es[0].close()
    es.close()
    ninstr = {e: len(P.ops[e]) for e in Prog.ENG}
    if P.trace is not None:
        for t in P.trace:
            print("OP", *t)
    return nc, ninstr


_CACHE = {}


def run(inputs, S, L):
    x_prompt = np.asarray(inputs["x_prompt"], np.float32)
    B = x_prompt.shape[0]
    key = (S, L)
    if key not in _CACHE:
        _CACHE[key] = build_program(S, L)
    nc, _ = _CACHE[key]
    g = lambda n: np.asarray(inputs[n], np.float32)
    wflat = build_wflat(L, g("w_ffn_up"), g("w_ffn_down"), g("w_in"), g("w_branch"), g("w_out"),
                        g("w_ple_up"), g("w_ple_gate"), g("pool_w"))
    wflat = wflat.reshape(-1, 128)
    vecs = build_vecs(L, g("norm_g"), g("subln_g"), g("pool_scale"), g("conv_w"))
    lamb = np.ascontiguousarray(np.broadcast_to(g("lam")[:L].reshape(1, L * 256), (128, L * 256)))
    cosT, sinT = rope_tables(np.arange(S))
    cs, ss = rope_tables(PAST + np.arange(32))
    cosS = np.ascontiguousarray(np.concatenate([cs, cs], axis=1))
    sinS = np.ascontiguousarray(np.concatenate([ss, ss], axis=1))
    invc = np.empty((128, 64), np.float32)
    for gi, w in enumerate((2, 4, 8, 16)):
        invc[:, gi * 16:(gi + 1) * 16] = (1.0 / np.minimum(np.arange(16) + 1, w)).astype(np.float32)[None, :]
    ident = np.eye(128, dtype=np.float32)
    xs_all = g("x_sample"); ps_all = g("p_sample"); ck_all = g("cache_k"); cv_all = g("cache_v")
    sc_all = g("state_conv"); sp_all = g("state_pool"); pp_all = g("p_prompt")
    in_maps = []
    for c in range(8):
        b = c % B
        sl = slice(2 * c, 2 * c + 2)
        in_maps.append({
            "xp": np.ascontiguousarray(x_prompt[b, :S]),
            "pp": np.ascontiguousarray(pp_all[:L, b, :S]),
            "xs": np.ascontiguousarray(xs_all[sl].reshape(64, D)),
            "pps": np.ascontiguousarray(ps_all[:L, sl].reshape(L, 64, PLE)),
            "ck": np.ascontiguousarray(ck_all[:L, sl].reshape(L, 2, PAST, D)),
            "cv": np.ascontiguousarray(cv_all[:L, sl].reshape(L, 2, PAST, D)),
            "sc": np.ascontiguousarray(sc_all[:L, sl]),
            "spl": np.ascontiguousarray(sp_all[:L, sl]),
            "wflat": wflat, "vecs": vecs, "lamb": lamb,
            "cosT": cosT, "sinT": sinT, "cosS": cosS, "sinS": sinS, "invc": invc, "ident": ident,
        })
    import os
    NCR = int(os.environ.get("KCORES", "8"))
    res = run_bass_kernel_spmd(nc, in_maps[:NCR], core_ids=list(range(NCR)))
    R = list(res.results) + [res.results[i % NCR] for i in range(NCR, 8)]
    y = np.stack([R[b]["y"] for b in range(B)])
    ys = np.concatenate([R[c]["ys"].reshape(2, 32, D) for c in range(8)], axis=0)
    kp = np.stack([R[b]["ko"] for b in range(B)], axis=1).reshape(L, B, S, 16, 64)
    vp = np.stack([R[b]["vo"] for b in range(B)], axis=1).reshape(L, B, S, 8, 128)
    cp = np.stack([R[b]["co"] for b in range(B)], axis=1)
    ppo = np.stack([R[b]["po"] for b in range(B)], axis=1)
    ks = np.concatenate([R[c]["kso"].reshape(L, 2, 32, 16, 64) for c in range(8)], axis=1)
    vs = np.concatenate([R[c]["vso"].reshape(L, 2, 32, 8, 128) for c in range(8)], axis=1)
    cso = np.concatenate([R[c]["cso"] for c in range(8)], axis=1)
    pso = np.concatenate([R[c]["pso"] for c in range(8)], axis=1)
    return (y, ys, kp, vp, cp, ppo, ks, vs, cso, pso)


def kernel(**inputs):
    S = int(np.asarray(inputs["x_prompt"]).shape[1])
    L = int(np.asarray(inputs["norm_g"]).shape[0])
    return run(inputs, S, L)
```
